# Optimizing a Trainium2 kernel written in Bass

```python
import math
import jax, jax.numpy as jnp
from jax import lax
import numpy as np

D_MODEL = 2048
BATCH = 2
SEQ = 8192
DEPTH = 4

CHUNK = 64
GDN_HEAD_DIM = 128
GDN_WIDTH = D_MODEL // 2
GDN_HEADS = GDN_WIDTH // GDN_HEAD_DIM
SC_WIDTH = D_MODEL - GDN_WIDTH
GDN_CONV = 4
SC_CONV = 3
FFN_CONV = 3
N_MEM = 256
XATTN_HEADS = 4
XATTN_HEAD_DIM = D_MODEL // XATTN_HEADS
D_FF = ((8 * D_MODEL // 3 + 255) // 256) * 256
N_MIX_IN = 4 * GDN_WIDTH + 2 * GDN_HEADS + 3 * SC_WIDTH
EPS = 1e-6

kernel_name = 'hybrid_gdn_shortconv_memxattn_convffn'


def rmsnorm(x, w):
    xf = x.astype(jnp.float32)
    y = xf * lax.rsqrt(jnp.mean(xf * xf, axis=-1, keepdims=True) + EPS)
    return (y * w.astype(jnp.float32)).astype(x.dtype)


def l2norm(x):
    return x * lax.rsqrt(jnp.sum(x * x, axis=-1, keepdims=True) + EPS)


def causal_dwconv(x, w):
    K = w.shape[0]
    S = x.shape[1]
    w = w.astype(x.dtype)
    xp = jnp.pad(x, ((0, 0), (K - 1, 0), (0, 0)))
    y = xp[:, 0:S] * w[0]
    for j in range(1, K):
        y = y + xp[:, j:j + S] * w[j]
    return y


def gated_delta_rule(q, k, v, g, beta):
    Bsz, S, H, DK = q.shape
    DV = v.shape[-1]
    N = S // CHUNK

    def to_chunks(t):
        t = t.reshape((Bsz, N, CHUNK, H) + t.shape[3:])
        return jnp.moveaxis(t, 3, 1)

    q, k, v, g, beta = (to_chunks(t) for t in (q, k, v, g, beta))
    q = q * (DK ** -0.5)
    g = jnp.cumsum(g, axis=-1)
    causal = jnp.tril(jnp.ones((CHUNK, CHUNK), dtype=bool))
    strict = jnp.tril(jnp.ones((CHUNK, CHUNK), dtype=bool), k=-1)
    decay = jnp.exp(jnp.where(causal, g[..., :, None] - g[..., None, :], -jnp.inf))
    k_beta = k * beta[..., None]
    a_strict = jnp.where(strict, jnp.einsum('bhncd,bhnmd->bhncm', k_beta, k) * decay, 0.0)
    eye = jnp.eye(CHUNK, dtype=jnp.float32)
    rhs = jnp.concatenate([v * beta[..., None], k_beta * jnp.exp(g)[..., None]], axis=-1)
    sol = lax.linalg.triangular_solve(eye + a_strict, rhs, left_side=True, lower=True,
                                      unit_diagonal=True)
    u, w = sol[..., :DV], sol[..., DV:]
    attn = jnp.einsum('bhncd,bhnmd->bhncm', q, k) * decay
    q_dec = q * jnp.exp(g)[..., None]
    g_last = g[..., -1]
    k_dec = k * jnp.exp(g_last[..., None] - g)[..., None]

    def step(state, xs):
        q_i, k_i, u_i, w_i, attn_i, gl_i = xs
        v_new = u_i - jnp.einsum('bhcd,bhde->bhce', w_i, state)
        o_i = (jnp.einsum('bhcd,bhde->bhce', q_i, state)
               + jnp.einsum('bhcm,bhme->bhce', attn_i, v_new))
        state = (state * jnp.exp(gl_i)[..., None, None]
                 + jnp.einsum('bhcd,bhce->bhde', k_i, v_new))
        return state, o_i

    xs = tuple(jnp.moveaxis(t, 2, 0) for t in (q_dec, k_dec, u, w, attn, g_last))
    state0 = jnp.zeros((Bsz, H, DK, DV), jnp.float32)
    _, o = lax.scan(step, state0, xs)
    return jnp.transpose(o, (1, 0, 3, 2, 4)).reshape(Bsz, S, H, DV)


def gdn_group(proj, conv_w, a_log, dt_bias, out_gain):
    Bsz, S, _ = proj.shape
    W, H, Dh = GDN_WIDTH, GDN_HEADS, GDN_HEAD_DIM
    qkv = jax.nn.silu(causal_dwconv(proj[..., :3 * W], conv_w)).astype(jnp.float32)
    q = l2norm(qkv[..., :W].reshape(Bsz, S, H, Dh))
    k = l2norm(qkv[..., W:2 * W].reshape(Bsz, S, H, Dh))
    v = qkv[..., 2 * W:].reshape(Bsz, S, H, Dh)
    z = proj[..., 3 * W:4 * W].reshape(Bsz, S, H, Dh)
    b_raw = proj[..., 4 * W:4 * W + H].astype(jnp.float32)
    a_raw = proj[..., 4 * W + H:4 * W + 2 * H].astype(jnp.float32)
    beta = jax.nn.sigmoid(b_raw)
    g = -jnp.exp(a_log.astype(jnp.float32)) * jax.nn.softplus(a_raw + dt_bias.astype(jnp.float32))
    o = gated_delta_rule(q, k, v, g, beta)
    o = rmsnorm(o, out_gain).astype(proj.dtype) * jax.nn.silu(z)
    return o.reshape(Bsz, S, W)


def shortconv_group(proj, conv_w):
    off = 4 * GDN_WIDTH + 2 * GDN_HEADS
    b_gate = proj[..., off:off + SC_WIDTH]
    c_gate = proj[..., off + SC_WIDTH:off + 2 * SC_WIDTH]
    h = proj[..., off + 2 * SC_WIDTH:off + 3 * SC_WIDTH]
    return b_gate * causal_dwconv(c_gate * h, conv_w)


def memory_xattn(h, mem_n, w_q, w_k, w_v, w_o):
    Bsz, S, _ = h.shape
    q = (h @ w_q).reshape(Bsz, S, XATTN_HEADS, XATTN_HEAD_DIM)
    k = (mem_n @ w_k).reshape(Bsz, N_MEM, XATTN_HEADS, XATTN_HEAD_DIM)
    v = (mem_n @ w_v).reshape(Bsz, N_MEM, XATTN_HEADS, XATTN_HEAD_DIM)
    s = jnp.einsum('bshd,bmhd->bhsm', q, k).astype(jnp.float32) * (XATTN_HEAD_DIM ** -0.5)
    p = jax.nn.softmax(s, axis=-1).astype(v.dtype)
    o = jnp.einsum('bhsm,bmhd->bshd', p, v).reshape(Bsz, S, D_MODEL)
    return o @ w_o


def conv_ffn(h, w_up, conv_w, w_down):
    u = causal_dwconv(h @ w_up, conv_w)
    gate, up = u[..., :D_FF], u[..., D_FF:]
    return (jax.nn.silu(gate) * up) @ w_down


def setup_inputs(seed: int = 0) -> dict:
    key = jax.random.key(seed)
    ks = jax.random.split(key, 24)
    L, D = DEPTH, D_MODEL
    out_scale = (3 * DEPTH) ** -0.5

    def normal(k, shape, std):
        return jax.random.normal(k, shape, jnp.float32) * std

    def gain(k, shape):
        return 1.0 + normal(k, shape, 0.02)

    dt = jnp.exp(jax.random.uniform(ks[6], (L, GDN_HEADS), jnp.float32,
                                    math.log(1e-3), math.log(1e-1)))
    return {
        'x': normal(ks[0], (BATCH, SEQ, D), 1.0),
        'mem': normal(ks[1], (BATCH, N_MEM, D), 1.0),
        'mix_norm': gain(ks[2], (L, D)),
        'w_mix_in': normal(ks[3], (L, D, N_MIX_IN), D ** -0.5),
        'gdn_conv': normal(ks[4], (L, GDN_CONV, 3 * GDN_WIDTH), GDN_CONV ** -0.5),
        'gdn_a_log': jnp.log(jax.random.uniform(ks[5], (L, GDN_HEADS), jnp.float32, 1.0, 16.0)),
        'gdn_dt_bias': dt + jnp.log(-jnp.expm1(-dt)),
        'gdn_out_norm': gain(ks[7], (L, GDN_HEAD_DIM)),
        'sc_conv': normal(ks[8], (L, SC_CONV, SC_WIDTH), SC_CONV ** -0.5),
        'w_mix_out': normal(ks[9], (L, D, D), D ** -0.5 * out_scale),
        'xattn_norm': gain(ks[10], (L, D)),
        'mem_norm': gain(ks[11], (L, D)),
        'w_xq': normal(ks[12], (L, D, D), D ** -0.5),
        'w_xk': normal(ks[13], (L, D, D), D ** -0.5),
        'w_xv': normal(ks[14], (L, D, D), D ** -0.5),
        'w_xo': normal(ks[15], (L, D, D), D ** -0.5 * out_scale),
        'ffn_norm': gain(ks[16], (L, D)),
        'w_ffn_up': normal(ks[17], (L, D, 2 * D_FF), D ** -0.5),
        'ffn_conv': normal(ks[18], (L, FFN_CONV, 2 * D_FF), FFN_CONV ** -0.5),
        'w_ffn_down': normal(ks[19], (L, D_FF, D), D_FF ** -0.5 * out_scale),
        'final_norm': gain(ks[20], (D,)),
    }


def reference(x, mem, mix_norm, w_mix_in, gdn_conv, gdn_a_log, gdn_dt_bias, gdn_out_norm,
              sc_conv, w_mix_out, xattn_norm, mem_norm, w_xq, w_xk, w_xv, w_xo,
              ffn_norm, w_ffn_up, ffn_conv, w_ffn_down, final_norm):
    for l in range(DEPTH):
        h = rmsnorm(x, mix_norm[l])
        proj = h @ w_mix_in[l]
        y_gdn = gdn_group(proj, gdn_conv[l], gdn_a_log[l], gdn_dt_bias[l], gdn_out_norm[l])
        y_sc = shortconv_group(proj, sc_conv[l])
        x = x + jnp.concatenate([y_gdn, y_sc], axis=-1) @ w_mix_out[l]
        h = rmsnorm(x, xattn_norm[l])
        mem_n = rmsnorm(mem, mem_norm[l])
        x = x + memory_xattn(h, mem_n, w_xq[l], w_xk[l], w_xv[l], w_xo[l])
        h = rmsnorm(x, ffn_norm[l])
        x = x + conv_ffn(h, w_ffn_up[l], ffn_conv[l], w_ffn_down[l])
    return rmsnorm(x, final_norm)
```

```python
import numpy as np
import concourse.bass as bass
import concourse.mybir as mybir
from concourse.bass_utils import run_bass_kernel_spmd
from contextlib import ExitStack

F32 = mybir.dt.float32
BF16 = mybir.dt.bfloat16
AF = mybir.ActivationFunctionType
ALU = mybir.AluOpType

D = 2048
B = 2
S = 8192
DEPTH = 4
NCORE = 8
KC = D // 128
H = 8
DH = 128
DFF = 5632
NMEM = 256
EPS = 1e-6
TJ = 1024
HALO = 3
NCOL = TJ + HALO
NJOB = 2
NMIX = 7184
NOC_A = 57
NEG = -60000.0


class Sched:
    COMPUTE = ('pe', 'act', 'dve', 'pool')
    NDMASEM = 6

    def __init__(self, nc, dma_queues=('sp', 'pool')):
        self.nc = nc
        self.ops = []
        self.last_w = {}
        self.readers = {}
        self.dma_queues = dma_queues

    def add(self, eng, fn, reads=(), writes=(), dma=False):
        i = len(self.ops)
        deps = set()
        for k in list(reads) + list(writes):
            if k in self.last_w:
                deps.add(self.last_w[k])
        for k in writes:
            for r in self.readers.get(k, ()):
                deps.add(r)
        deps.discard(i)
        self.ops.append(dict(eng=eng, fn=fn, deps=deps, dma=dma))
        for k in writes:
            self.last_w[k] = i
            self.readers[k] = []
        for k in reads:
            if k not in writes:
                self.readers.setdefault(k, []).append(i)
        return i

    def emit(self, final_wait_ops=()):
        nc = self.nc
        ops = self.ops
        for o in ops:
            o['sig'] = o['dma']
        for o in ops:
            if o['eng'] == 'pe' and not o['dma']:
                o['deps'] = {d for d in o['deps'] if not (ops[d]['eng'] == 'pe' and not ops[d]['dma'])}
        for o in ops:
            for d in o['deps']:
                if not ops[d]['dma']:
                    ops[d]['sig'] = True
        with ExitStack() as es:
            csem = {e: es.enter_context(nc.semaphore('c_' + e)) for e in self.COMPUTE}
            dsem = {q: [es.enter_context(nc.semaphore('d_%s%d' % (q, j))) for j in range(self.NDMASEM)]
                    for q in self.dma_queues}
            ccount = {e: 0 for e in self.COMPUTE}
            dcount = {q: [0] * self.NDMASEM for q in self.dma_queues}
            drr = {q: 0 for q in self.dma_queues}
            for o in ops:
                if o['dma']:
                    q = o['eng']
                    j = drr[q]
                    drr[q] = (j + 1) % self.NDMASEM
                    o['prev_same_sem'] = dcount[q][j]
                    dcount[q][j] += 16
                    o['signal'] = (dsem[q][j], dcount[q][j], ('d', q, j))
                elif o['sig']:
                    e = o['eng']
                    ccount[e] += 1
                    o['signal'] = (csem[e], ccount[e], ('c', e))
                else:
                    o['signal'] = None
            streams = {}
            for i, o in enumerate(ops):
                streams.setdefault(o['eng'], []).append(i)
            waited = {}

            def gen(engname):
                def body(eng):
                    for i in streams.get(engname, []):
                        o = ops[i]
                        need = {}
                        for d in o['deps']:
                            po = ops[d]
                            if po['signal'] is None:
                                continue
                            sem, val, key = po['signal']
                            if key not in need or need[key][1] < val:
                                need[key] = (sem, val)
                        if o['dma'] and o['prev_same_sem'] > 0:
                            sem, val, key = o['signal']
                            pv = o['prev_same_sem']
                            if key not in need or need[key][1] < pv:
                                need[key] = (sem, pv)
                        for key, (sem, val) in need.items():
                            if waited.get((engname, key), 0) >= val:
                                continue
                            waited[(engname, key)] = val
                            eng.wait_ge(sem, val)
                        ins = o['fn'](eng)
                        if o['signal'] is not None:
                            sem, val, key = o['signal']
                            ins.then_inc(sem, 16 if o['dma'] else 1)
                    if engname == 'sp':
                        for d in final_wait_ops:
                            sem, val, key = ops[d]['signal']
                            eng.wait_ge(sem, val)
                return body

            with nc.Block() as block:
                block.sync(gen('sp'))
                block.tensor(gen('pe'))
                block.scalar(gen('act'))
                block.vector(gen('dve'))
                block.gpsimd(gen('pool'))


class Ctx:
    def __init__(self, nc, es):
        self.nc = nc
        self.es = es
        self.S = Sched(nc)
        self.banks = [es.enter_context(nc.psum_tensor('psb%d' % i, [128, 512], F32)) for i in range(8)]
        self.bank_rr = 0
        self.out_dmas = []

    def sb(self, name, shape, dt):
        return self.es.enter_context(self.nc.sbuf_tensor(name, shape, dt))

    def next_bank(self):
        b = self.bank_rr
        self.bank_rr = (b + 1) % 8
        return b

    def dma_in(self, q, out_ap, in_ap, wkeys, rkeys=()):
        return self.S.add(q, lambda e: e.dma_start(out=out_ap, in_=in_ap), reads=rkeys, writes=wkeys, dma=True)

    def dma_out(self, out_ap, in_ap, rkeys, q='sp'):
        i = self.S.add(q, lambda e: e.dma_start(out=out_ap, in_=in_ap), reads=rkeys, dma=True)
        self.out_dmas.append(i)
        return i

    def finish(self):
        self.S.emit(final_wait_ops=self.out_dmas)


def col_tiles(ncol):
    t = []
    c = 0
    while c < ncol:
        n = min(512, ncol - c)
        t.append((c, n))
        c += n
    return t


def make_consts(cx):
    nc = cx.nc
    ones = cx.sb('ones_f', [128, 128], F32)
    cx.S.add('pool', lambda e: e.memset(ones[:], 1.0), writes=['ones'])
    return ones


def emit_rmsnorm(cx, xall, xkey, wn, wnkey, hout, hkeys, ones, ncol, sq_bufs, rn_bufs, out_fn=None):
    S = cx.S
    for ti, (c0, n) in enumerate(col_tiles(ncol)):
        bk = cx.next_bank()
        ps = cx.banks[bk]
        for kc in range(KC):
            sqb = sq_bufs[kc % len(sq_bufs)]
            sqk = 'rms_sq%d' % (kc % len(sq_bufs))
            S.add('act', lambda e, sqb=sqb, kc=kc, c0=c0, n=n: e.activation(out=sqb[:, :n], in_=xall[:, kc, c0:c0 + n], func=AF.Square),
                  reads=[xkey], writes=[sqk])
            S.add('pe', lambda e, sqb=sqb, kc=kc, n=n, ps=ps: e.matmul(ps[:, :n], ones[:], sqb[:, :n], start=(kc == 0), stop=(kc == KC - 1)),
                  reads=[sqk, 'ones'], writes=['bank%d' % bk])
        rn = rn_bufs[ti % len(rn_bufs)]
        rnk = 'rms_rn%d' % (ti % len(rn_bufs))
        S.add('act', lambda e, rn=rn, ps=ps, n=n: e.activation(out=rn[:, :n], in_=ps[:, :n], func=AF.Ln, scale=1.0 / D, bias=EPS),
              reads=['bank%d' % bk], writes=[rnk])
        S.add('act', lambda e, rn=rn, n=n: e.activation(out=rn[:, :n], in_=rn[:, :n], func=AF.Exp, scale=-0.5),
              reads=[rnk], writes=[rnk])
        for kc in range(KC):
            if out_fn is not None:
                out_fn(kc, c0, n, rn, rnk)
                continue
            S.add('dve', lambda e, kc=kc, c0=c0, n=n, rn=rn: e.scalar_tensor_tensor(
                out=hout[:, kc, c0:c0 + n], in0=xall[:, kc, c0:c0 + n], scalar=wn[:, kc:kc + 1], in1=rn[:, :n],
                op0=ALU.mult, op1=ALU.mult),
                reads=[xkey, rnk, wnkey], writes=[hkeys[kc]])


def emit_proj_chunk(cx, wsrc_ap, wbufs, widx, hin, hkeys, nk, ncol, stg, stgkey, m=128, wname='wA', evac=None, c_off=0):
    S = cx.S
    slot = widx % len(wbufs)
    wb = wbufs[slot]
    wk = '%s_%d' % (wname, slot)
    cx.dma_in('pool', wb[:, 0:nk, :], wsrc_ap, [wk])
    for (c0, n) in col_tiles(ncol):
        bk = cx.next_bank()
        ps = cx.banks[bk]
        for kc in range(nk):
            S.add('pe', lambda e, kc=kc, c0=c0, n=n, ps=ps: e.matmul(ps[:m, :n], wb[:, kc, 0:m], hin[:, kc, c_off + c0:c_off + c0 + n],
                                                                 start=(kc == 0), stop=(kc == nk - 1)),
                  reads=[wk, hkeys[kc]], writes=['bank%d' % bk])
        if evac is not None:
            evac(ps, 'bank%d' % bk, c0, n)
        else:
            S.add('act', lambda e, c0=c0, n=n, ps=ps: e.activation(out=stg[:m, c0:c0 + n], in_=ps[:m, :n], func=AF.Copy),
                  reads=['bank%d' % bk], writes=[stgkey])


def emit_conv(cx, out_ap_fn, src, srckey, wt, wcol0, K, nout, outkey, col_shift):
    S = cx.S
    for j in range(K):
        if j == 0:
            S.add('dve', lambda e, j=j: e.tensor_scalar(out=out_ap_fn(), in0=src[:, col_shift + j: col_shift + j + nout],
                                                        scalar1=wt[:, wcol0 + j: wcol0 + j + 1], scalar2=None, op0=ALU.mult),
                  reads=[srckey, 'convw'], writes=[outkey])
        else:
            S.add('dve', lambda e, j=j: e.scalar_tensor_tensor(out=out_ap_fn(), in0=src[:, col_shift + j: col_shift + j + nout],
                                                               scalar=wt[:, wcol0 + j: wcol0 + j + 1], in1=out_ap_fn(),
                                                               op0=ALU.mult, op1=ALU.add),
                  reads=[srckey, 'convw', outkey], writes=[outkey])


def build_phase_a():
    nc = bass.Bass("TRN2", target_bir_lowering=False)
    xT = nc.dram_tensor("xT", [NJOB, 128, KC, NCOL], F32, kind="ExternalInput").ap()
    wn_d = nc.dram_tensor("wn", [128, KC], F32, kind="ExternalInput").ap()
    W_d = nc.dram_tensor("W", [NOC_A, 128, KC, 128], F32, kind="ExternalInput").ap()
    gconv_d = nc.dram_tensor("gconv", [128, 24 * 4], F32, kind="ExternalInput").ap()
    scconv_d = nc.dram_tensor("scconv", [128, 8 * 3], F32, kind="ExternalInput").ap()
    small_d = nc.dram_tensor("small", [16, 2], F32, kind="ExternalInput").ap()
    qkv_o = nc.dram_tensor("qkv", [NJOB, 24, 128, TJ], F32, kind="ExternalOutput").ap()
    sz_o = nc.dram_tensor("sz", [NJOB, 8, 128, TJ], F32, kind="ExternalOutput").ap()
    bg_o = nc.dram_tensor("bg", [NJOB, 2, 16, TJ], F32, kind="ExternalOutput").ap()
    ysc_o = nc.dram_tensor("ysc", [NJOB, 8, 128, TJ], F32, kind="ExternalOutput").ap()
    with ExitStack() as es:
        cx = Ctx(nc, es)
        S = cx.S
        ones = make_consts(cx)
        xall = cx.sb('xall', [128, KC, NCOL], F32)
        hbf = cx.sb('hbf', [128, KC, NCOL], BF16)
        wn = cx.sb('wn_sb', [128, KC], F32)
        gconv = cx.sb('gconv_sb', [128, 96], F32)
        scconv = cx.sb('scconv_sb', [128, 24], F32)
        small = cx.sb('small_sb', [16, 2], F32)
        nexpa = cx.sb('nexpa', [16, 1], F32)
        wbufs = [cx.sb('wbA%d' % i, [128, KC, 128], BF16) for i in range(3)]
        stgs = [cx.sb('stg%d' % i, [128, NCOL], F32) for i in range(4)]
        sqb = [cx.sb('sqb%d' % i, [128, 512], F32) for i in range(2)]
        rnb = [cx.sb('rnb%d' % i, [128, 512], F32) for i in range(2)]
        accs = [cx.sb('acc%d' % i, [128, TJ], F32) for i in range(2)]
        sbufs = [cx.sb('sil%d' % i, [128, TJ], F32) for i in range(2)]
        sq2 = cx.sb('sq2', [128, TJ], F32)
        rn2 = cx.sb('rn2', [128, TJ], F32)
        obufs = [cx.sb('ob%d' % i, [128, TJ], F32) for i in range(3)]
        cx.dma_in('sp', wn[:], wn_d, ['na_wn'])
        cx.dma_in('sp', gconv[:], gconv_d, ['convw'])
        cx.dma_in('sp', scconv[:], scconv_d, ['convw2'])
        cx.dma_in('sp', small[:], small_d, ['small'])
        S.add('act', lambda e: e.activation(out=nexpa[:], in_=small[:, 0:1], func=AF.Exp), reads=['small'], writes=['nexpa'])
        S.add('dve', lambda e: e.tensor_scalar(out=nexpa[:], in0=nexpa[:], scalar1=-1.0, scalar2=None, op0=ALU.mult),
              reads=['nexpa'], writes=['nexpa'])
        hkeys = ['na_h%d' % kc for kc in range(KC)]
        occ = [0]
        stg_rr = [0]
        ob_rr = [0]

        def next_stg():
            i = stg_rr[0]
            stg_rr[0] = (i + 1) % len(stgs)
            return stgs[i], 'stg%d' % i

        def next_ob():
            i = ob_rr[0]
            ob_rr[0] = (i + 1) % len(obufs)
            return obufs[i], 'ob%d' % i

        for job in range(NJOB):
            for kc in range(KC):
                cx.dma_in('sp', xall[:, kc, :], xT[job, :, kc, :], ['na_x'])
            emit_rmsnorm(cx, xall, 'na_x', wn, 'na_wn', hbf, hkeys, ones, NCOL, sqb, rnb)

            def proj(oc, m=128):
                stg, sk = next_stg()
                emit_proj_chunk(cx, W_d[oc], wbufs, occ[0], hbf, hkeys, KC, NCOL, stg, sk, m=m)
                occ[0] += 1
                return stg, sk

            for oc in range(24):
                stg, sk = proj(oc)
                acc = accs[oc % 2]
                ak = 'acc%d' % (oc % 2)
                emit_conv(cx, lambda acc=acc: acc[:, :], stg, sk, gconv, oc * 4, 4, TJ, ak, 0)
                sl = sbufs[oc % 2]
                slk = 'sil%d' % (oc % 2)
                S.add('act', lambda e, sl=sl, acc=acc: e.activation(out=sl[:], in_=acc[:], func=AF.Silu), reads=[ak], writes=[slk])
                if oc < 16:
                    S.add('pool', lambda e, sl=sl: e.tensor_tensor(out=sq2[:], in0=sl[:], in1=sl[:], op=ALU.mult), reads=[slk], writes=['sq2'])
                    bks = []
                    for (c0, n) in col_tiles(TJ):
                        bk = cx.next_bank()
                        bks.append((bk, c0, n))
                        S.add('pe', lambda e, bk=bk, c0=c0, n=n: e.matmul(cx.banks[bk][:, :n], ones[:], sq2[:, c0:c0 + n], start=True, stop=True),
                              reads=['sq2', 'ones'], writes=['bank%d' % bk])
                    for (bk, c0, n) in bks:
                        S.add('act', lambda e, bk=bk, c0=c0, n=n: e.activation(out=rn2[:, c0:c0 + n], in_=cx.banks[bk][:, :n], func=AF.Ln, bias=EPS),
                              reads=['bank%d' % bk], writes=['rn2_%d' % c0])
                        S.add('act', lambda e, c0=c0, n=n: e.activation(out=rn2[:, c0:c0 + n], in_=rn2[:, c0:c0 + n], func=AF.Exp, scale=-0.5),
                              reads=['rn2_%d' % c0], writes=['rn2_%d' % c0])
                    ob, obk = next_ob()
                    qs = (DH ** -0.5) if oc < 8 else 1.0
                    S.add('dve', lambda e, ob=ob, sl=sl, qs=qs: e.scalar_tensor_tensor(out=ob[:], in0=sl[:], scalar=qs, in1=rn2[:], op0=ALU.mult, op1=ALU.mult),
                          reads=[slk] + ['rn2_%d' % c0 for (c0, n) in col_tiles(TJ)], writes=[obk])
                    cx.dma_out(qkv_o[job, oc], ob[:], [obk])
                else:
                    cx.dma_out(qkv_o[job, oc], sl[:], [slk])
            for j in range(8):
                stg, sk = proj(24 + j)
                ob, obk = next_ob()
                S.add('act', lambda e, ob=ob, stg=stg: e.activation(out=ob[:], in_=stg[:, HALO:], func=AF.Silu), reads=[sk], writes=[obk])
                cx.dma_out(sz_o[job, j], ob[:], [obk])
            stg, sk = proj(32, m=16)
            ob, obk = next_ob()
            S.add('act', lambda e, ob=ob, stg=stg: e.activation(out=ob[0:16, :], in_=stg[0:16, HALO:], func=AF.Sigmoid), reads=[sk], writes=[obk])
            cx.dma_out(bg_o[job, 0], ob[0:16, :], [obk])
            ob2, obk2 = next_ob()
            S.add('act', lambda e, ob2=ob2, stg=stg: e.activation(out=ob2[0:16, :], in_=stg[0:16, HALO:], func=AF.Exp, bias=small[:, 1:2]),
                  reads=[sk, 'small'], writes=[obk2])
            S.add('act', lambda e, ob2=ob2: e.activation(out=ob2[0:16, :], in_=ob2[0:16, :], func=AF.Ln, bias=1.0), reads=[obk2], writes=[obk2])
            S.add('dve', lambda e, ob2=ob2: e.tensor_scalar(out=ob2[0:16, :], in0=ob2[0:16, :], scalar1=nexpa[:, 0:1], scalar2=None, op0=ALU.mult),
                  reads=[obk2, 'nexpa'], writes=[obk2])
            cx.dma_out(bg_o[job, 1], ob2[0:16, :], [obk2])
            for j in range(8):
                stc, skc = proj(41 + j)
                sth, skh = proj(49 + j)
                S.add('pool', lambda e, stc=stc, sth=sth: e.tensor_tensor(out=stc[:], in0=stc[:], in1=sth[:], op=ALU.mult),
                      reads=[skc, skh], writes=[skc])
                acc = accs[j % 2]
                ak = 'acc%d' % (j % 2)
                for jj in range(3):
                    if jj == 0:
                        S.add('dve', lambda e, acc=acc, stc=stc, j=j, jj=jj: e.tensor_scalar(
                            out=acc[:], in0=stc[:, 1 + jj:1 + jj + TJ], scalar1=scconv[:, j * 3 + jj:j * 3 + jj + 1], scalar2=None, op0=ALU.mult),
                            reads=[skc, 'convw2'], writes=[ak])
                    else:
                        S.add('dve', lambda e, acc=acc, stc=stc, j=j, jj=jj: e.scalar_tensor_tensor(
                            out=acc[:], in0=stc[:, 1 + jj:1 + jj + TJ], scalar=scconv[:, j * 3 + jj:j * 3 + jj + 1], in1=acc[:],
                            op0=ALU.mult, op1=ALU.add),
                            reads=[skc, 'convw2', ak], writes=[ak])
                stb, skb = proj(33 + j)
                ob, obk = next_ob()
                S.add('dve', lambda e, ob=ob, acc=acc, stb=stb: e.tensor_tensor(out=ob[:], in0=acc[:], in1=stb[:, HALO:], op=ALU.mult),
                      reads=[ak, skb], writes=[obk])
                cx.dma_out(ysc_o[job, j], ob[:], [obk])
        cx.finish()
    return nc


def to_fm(a):
    n, dd = a.shape
    return np.ascontiguousarray(a.T.reshape(dd // 128, 128, n).transpose(1, 0, 2))


def job_cols(a_b, t0, halo=HALO, tj=TJ):
    if t0 - halo >= 0:
        return a_b[t0 - halo:t0 + tj]
    out = np.zeros((halo + tj,) + a_b.shape[1:], a_b.dtype)
    out[halo - t0:] = a_b[0:t0 + tj]
    return out


def wchunks(w, col_lists):
    K = w.shape[0]
    out = np.zeros((len(col_lists), 128, K // 128, 128), np.float32)
    for i, (c0, n) in enumerate(col_lists):
        out[i, :, :, :n] = w[:, c0:c0 + n].reshape(K // 128, 128, n).transpose(1, 0, 2)
    return out


def phase_a_cols():
    cols = [(oc * 128, 128) for oc in range(32)]
    cols.append((4096, 16))
    off = 4112
    for g in range(3):
        cols += [(off + g * 1024 + j * 128, 128) for j in range(8)]
    return cols


def phase_a_params(inp, l):
    W = wchunks(inp['w_mix_in'][l], phase_a_cols())
    gconv = np.ascontiguousarray(inp['gdn_conv'][l].reshape(4, 24, 128).transpose(2, 1, 0)).reshape(128, 96)
    scconv = np.ascontiguousarray(inp['sc_conv'][l].reshape(3, 8, 128).transpose(2, 1, 0)).reshape(128, 24)
    small = np.zeros((16, 2), np.float32)
    small[8:, 0] = inp['gdn_a_log'][l]
    small[8:, 1] = inp['gdn_dt_bias'][l]
    wn = np.ascontiguousarray(inp['mix_norm'][l].reshape(KC, 128).T)
    return dict(W=W, gconv=gconv, scconv=scconv, small=small, wn=wn)


def core_tokens(c):
    b = c // 4
    t0 = (c % 4) * (NJOB * TJ)
    return b, t0


GB = 8
NCH = S // 64
NHB = 2


def build_phase_b():
    nc = bass.Bass("TRN2", target_bir_lowering=False)
    qT_d = nc.dram_tensor("qT", [NHB, 128, S], F32, kind="ExternalInput").ap()
    kT_d = nc.dram_tensor("kT", [NHB, 128, S], F32, kind="ExternalInput").ap()
    k64_d = nc.dram_tensor("k64", [NHB, 64, NCH, 128], F32, kind="ExternalInput").ap()
    v64_d = nc.dram_tensor("v64", [NHB, 64, NCH, 128], F32, kind="ExternalInput").ap()
    g64_d = nc.dram_tensor("g64", [NHB, 64, NCH], F32, kind="ExternalInput").ap()
    b64_d = nc.dram_tensor("b64", [NHB, 64, NCH], F32, kind="ExternalInput").ap()
    o64_d = nc.dram_tensor("o64", [NHB, 64, NCH, 128], F32, kind="ExternalOutput").ap()
    W = GB * 64
    with ExitStack() as es:
        cx = Ctx(nc, es)
        S_ = cx.S
        sb = cx.sb
        ident = sb('ident', [64, GB, 64], F32)
        ones = sb('ones', [64, 128], F32)
        triu = sb('triu', [64, 64], F32)
        mLs = sb('mLs', [64, GB, 64], F32)
        mU = sb('mU', [64, GB, 64], F32)
        S_.add('pool', lambda e: e.memset(ident[:], 0.0), writes=['ident'])
        S_.add('pool', lambda e: e.affine_select(out=ident[:], in_=ident[:], pattern=[[0, GB], [-1, 64]], compare_op=ALU.not_equal,
                                                 fill=1.0, base=0, channel_multiplier=1), reads=['ident'], writes=['ident'])
        S_.add('pool', lambda e: e.memset(ones[:], 1.0), writes=['ones'])
        S_.add('pool', lambda e: e.memset(triu[:], 1.0), writes=['triu'])
        S_.add('pool', lambda e: e.affine_select(out=triu[:], in_=triu[:], pattern=[[1, 64]], compare_op=ALU.is_ge,
                                                 fill=0.0, base=0, channel_multiplier=-1), reads=['triu'], writes=['triu'])
        S_.add('pool', lambda e: e.memset(mLs[:], 0.0), writes=['mLs'])
        S_.add('pool', lambda e: e.affine_select(out=mLs[:], in_=mLs[:], pattern=[[0, GB], [-1, 64]], compare_op=ALU.is_gt,
                                                 fill=NEG, base=0, channel_multiplier=1), reads=['mLs'], writes=['mLs'])
        S_.add('pool', lambda e: e.memset(mU[:], 0.0), writes=['mU'])
        S_.add('pool', lambda e: e.affine_select(out=mU[:], in_=mU[:], pattern=[[0, GB], [1, 64]], compare_op=ALU.is_ge,
                                                 fill=NEG, base=0, channel_multiplier=-1), reads=['mU'], writes=['mU'])
        g64 = sb('g64_sb', [64, NCH], F32)
        b64 = sb('b64_sb', [64, NCH], F32)
        nb64 = sb('nb64', [64, NCH], F32)
        gc = sb('gc', [64, NCH], F32)
        eg = sb('eg', [64, NCH], F32)
        kdf = sb('kdf', [64, NCH], F32)
        bexp = sb('bexp', [64, NCH], F32)
        egl = sb('egl', [128, NCH], F32)
        kTg = [sb('kTg%d' % i, [128, W], F32) for i in range(2)]
        qTg = [sb('qTg%d' % i, [128, W], F32) for i in range(2)]
        k64g = [sb('k64g%d' % i, [64, GB, 128], F32) for i in range(2)]
        v64g = [sb('v64g%d' % i, [64, GB, 128], F32) for i in range(2)]
        diag = sb('diag', [64, GB, 64], F32)
        ndiag = sb('ndiag', [64, GB, 64], F32)
        dLs = sb('dLs', [64, GB, 64], F32)
        dU = sb('dU', [64, GB, 64], F32)
        Pb = [sb('Pb%d' % i, [64, GB, 64], F32) for i in range(2)]
        Qb = [sb('Qb%d' % i, [64, GB, 64], F32) for i in range(2)]
        X = sb('X', [64, GB, 64], F32)
        attnT = sb('attnT', [64, GB, 64], F32)
        vb = sb('vb', [64, GB, 128], F32)
        kbg = sb('kbg', [64, GB, 128], F32)
        kd = sb('kd', [64, GB, 128], F32)
        u_sb = sb('u_sb', [64, GB, 128], F32)
        wT_sb = sb('wT_sb', [128, W], F32)
        o_out = [sb('o_out%d' % i, [64, GB, 128], F32) for i in range(2)]
        o_tmp = [sb('o_tmp%d' % i, [64, 128], F32) for i in range(2)]
        vn = [sb('vn%d' % i, [64, 128], F32) for i in range(2)]
        Sst = [sb('Sst%d' % i, [128, 128], F32) for i in range(2)]
        bank_rr = [0]

        def gbank():
            b = bank_rr[0]
            bank_rr[0] = (b + 1) % 4
            return b, cx.banks[b], 'bank%d' % b

        def qbank(which, st):
            bkn = 4 + which
            qd = st % 4
            return cx.banks[bkn][:, qd * 128:(qd + 1) * 128], 'qb%d_%d' % (which, qd)

        def bc(ap2d, n):
            return ap2d.unsqueeze(2).to_broadcast([ap2d.shape[0], ap2d.shape[1], n])

        step = [0]
        import os
        STOP = float(os.environ.get('STOPB', '99'))

        class _Stop(Exception):
            pass

        def chk(n):
            if STOP <= n:
                raise _Stop()
        try:
          for hi in range(int(os.environ.get('NHD', NHB))):
              cx.dma_in('sp', g64[:], g64_d[hi], ['g64'])
              cx.dma_in('sp', b64[:], b64_d[hi], ['b64'])
              bk, ps, bkk = gbank()
              S_.add('pe', lambda e, ps=ps: e.matmul(ps[:64, :NCH], triu[:], g64[:], start=True, stop=True), reads=['triu', 'g64'], writes=[bkk])
              S_.add('dve', lambda e, ps=ps: e.tensor_copy(out=gc[:], in_=ps[:64, :NCH]), reads=[bkk], writes=['gc'])
              S_.add('act', lambda e: e.activation(out=eg[:], in_=gc[:], func=AF.Exp), reads=['gc'], writes=['eg'])
              bk2, ps2, bkk2 = gbank()
              S_.add('pe', lambda e, ps2=ps2: e.matmul(ps2[:, :NCH], ones[:], g64[:], start=True, stop=True), reads=['ones', 'g64'], writes=[bkk2])
              S_.add('act', lambda e, ps2=ps2: e.activation(out=egl[:], in_=ps2[:, :NCH], func=AF.Exp), reads=[bkk2], writes=['egl'])
              bk3, ps3, bkk3 = gbank()
              S_.add('pe', lambda e, ps3=ps3: e.matmul(ps3[:64, :NCH], ones[:, 0:64], g64[:], start=True, stop=True), reads=['ones', 'g64'], writes=[bkk3])
              S_.add('dve', lambda e, ps3=ps3: e.tensor_tensor(out=kdf[:], in0=ps3[:64, :NCH], in1=gc[:], op=ALU.subtract), reads=[bkk3, 'gc'], writes=['kdf'])
              S_.add('act', lambda e: e.activation(out=kdf[:], in_=kdf[:], func=AF.Exp), reads=['kdf'], writes=['kdf'])
              S_.add('dve', lambda e: e.tensor_tensor(out=bexp[:], in0=b64[:], in1=eg[:], op=ALU.mult), reads=['b64', 'eg'], writes=['bexp'])
              S_.add('dve', lambda e: e.tensor_scalar(out=nb64[:], in0=b64[:], scalar1=-1.0, scalar2=None, op0=ALU.mult), reads=['b64'], writes=['nb64'])
              S_.add('pool', lambda e: e.memset(Sst[step[0] % 2][:], 0.0), writes=['Sst%d' % (step[0] % 2)])
              chk(1)
              for gi in range(int(os.environ.get('NGRP', NCH // GB))):
                  ch0 = gi * GB
                  sl = gi % 2
                  kT_, qT_, k64_, v64_ = kTg[sl], qTg[sl], k64g[sl], v64g[sl]
                  kk = ['kTg%d' % sl, 'qTg%d' % sl, 'k64g%d' % sl, 'v64g%d' % sl]
                  cx.dma_in('sp', kT_[:], kT_d[hi, :, ch0 * 64:ch0 * 64 + W], [kk[0]])
                  cx.dma_in('sp', qT_[:], qT_d[hi, :, ch0 * 64:ch0 * 64 + W], [kk[1]])
                  cx.dma_in('pool', k64_[:], k64_d[hi, :, ch0:ch0 + GB, :], [kk[2]])
                  cx.dma_in('pool', v64_[:], v64_d[hi, :, ch0:ch0 + GB, :], [kk[3]])
                  gcs = gc[:, ch0:ch0 + GB]
                  S_.add('dve', lambda e, gcs=gcs: e.tensor_tensor(out=diag[:], in0=ident[:], in1=bc(gcs, 64), op=ALU.mult),
                         reads=['ident', 'gc'], writes=['diag'])
                  S_.add('pool', lambda e: e.tensor_scalar(out=ndiag[:], in0=diag[:], scalar1=-1.0, scalar2=None, op0=ALU.mult),
                         reads=['diag'], writes=['ndiag'])
                  bD, psD, kD = gbank()
                  for i in range(GB):
                      S_.add('pe', lambda e, i=i, psD=psD: e.matmul(psD[:64, i * 64:(i + 1) * 64], ones[:, 0:64], ndiag[:, i, :], start=True, stop=False),
                             reads=['ones', 'ndiag'], writes=[kD])
                      S_.add('pe', lambda e, i=i, psD=psD: e.matmul(psD[:64, i * 64:(i + 1) * 64], diag[:, i, :], ones[:, 0:64], start=False, stop=True),
                             reads=['ones', 'diag'], writes=[kD])
                  chk(2)
                  bK, psK, kK = gbank()
                  bA, psA, kA = gbank()
                  for i in range(GB):
                      S_.add('pe', lambda e, i=i, psK=psK, kT_=kT_: e.matmul(psK[:64, i * 64:(i + 1) * 64], kT_[:, i * 64:(i + 1) * 64], kT_[:, i * 64:(i + 1) * 64],
                                                                           start=True, stop=True), reads=[kk[0]], writes=[kK])
                  for i in range(GB):
                      S_.add('pe', lambda e, i=i, psA=psA, kT_=kT_, qT_=qT_: e.matmul(psA[:64, i * 64:(i + 1) * 64], kT_[:, i * 64:(i + 1) * 64], qT_[:, i * 64:(i + 1) * 64],
                                                                                    start=True, stop=True), reads=[kk[0], kk[1]], writes=[kA])
                  flat = lambda t: t[:].rearrange("p g c -> p (g c)")
                  S_.add('dve', lambda e, psD=psD: e.tensor_tensor(out=flat(dLs), in0=psD[:64, :], in1=flat(mLs), op=ALU.add), reads=[kD, 'mLs'], writes=['dLs'])
                  S_.add('act', lambda e: e.activation(out=flat(dLs), in_=flat(dLs), func=AF.Exp), reads=['dLs'], writes=['dLs'])
                  S_.add('dve', lambda e, psD=psD: e.scalar_tensor_tensor(out=flat(dU), in0=psD[:64, :], scalar=-1.0, in1=flat(mU), op0=ALU.mult, op1=ALU.add),
                         reads=[kD, 'mU'], writes=['dU'])
                  S_.add('act', lambda e: e.activation(out=flat(dU), in_=flat(dU), func=AF.Exp), reads=['dU'], writes=['dU'])
                  chk(3)
                  S_.add('dve', lambda e, psK=psK: e.tensor_tensor(out=flat(dLs), in0=psK[:64, :], in1=flat(dLs), op=ALU.mult), reads=[kK, 'dLs'], writes=['dLs'])
                  nbs = nb64[:, ch0:ch0 + GB]
                  S_.add('dve', lambda e, nbs=nbs: e.tensor_tensor(out=Qb[0][:], in0=dLs[:], in1=bc(nbs, 64), op=ALU.mult), reads=['dLs', 'nb64'], writes=['Qb0'])
                  S_.add('dve', lambda e, psA=psA: e.tensor_tensor(out=flat(attnT), in0=psA[:64, :], in1=flat(dU), op=ALU.mult), reads=[kA, 'dU'], writes=['attnT'])
                  chk(3.2)
                  bT, psT, kT = gbank()
                  for i in range(GB):
                      S_.add('pe', lambda e, i=i, psT=psT: e.matmul(psT[:64, i * 64:(i + 1) * 64], Qb[0][:, i, :], ident[:, 0, :], start=True, stop=True),
                             reads=['Qb0', 'ident'], writes=[kT])
                  chk(3.5)
                  S_.add('act', lambda e, psT=psT: e.activation(out=flat(Pb[0]), in_=psT[:64, :], func=AF.Copy), reads=[kT], writes=['Pb0'])
                  chk(3.7)
                  S_.add('dve', lambda e: e.tensor_tensor(out=flat(X), in0=flat(Pb[0]), in1=flat(ident), op=ALU.add), reads=['Pb0', 'ident'], writes=['X'])
                  chk(4)
                  cur = 0
                  for lev in range(1, 6):
                      nxt = 1 - cur
                      Pc, Qc, Pn, Qn = Pb[cur], Qb[cur], Pb[nxt], Qb[nxt]
                      kPc, kQc, kPn, kQn = 'Pb%d' % cur, 'Qb%d' % cur, 'Pb%d' % nxt, 'Qb%d' % nxt
                      bQ, psQ, kQ = gbank()
                      for i in range(GB):
                          S_.add('pe', lambda e, i=i, psQ=psQ, Pc=Pc, Qc=Qc: e.matmul(psQ[:64, i * 64:(i + 1) * 64], Pc[:, i, :], Qc[:, i, :], start=True, stop=True),
                                 reads=[kPc, kQc], writes=[kQ])
                      if lev < 5:
                          bP, psP, kP = gbank()
                          for i in range(GB):
                              S_.add('pe', lambda e, i=i, psP=psP, Pc=Pc, Qc=Qc: e.matmul(psP[:64, i * 64:(i + 1) * 64], Qc[:, i, :], Pc[:, i, :], start=True, stop=True),
                                     reads=[kPc, kQc], writes=[kP])
                      S_.add('act', lambda e, psQ=psQ, Qn=Qn: e.activation(out=flat(Qn), in_=psQ[:64, :], func=AF.Copy), reads=[kQ], writes=[kQn])
                      if lev < 5:
                          S_.add('act', lambda e, psP=psP, Pn=Pn: e.activation(out=flat(Pn), in_=psP[:64, :], func=AF.Copy), reads=[kP], writes=[kPn])
                      bX, psX, kX = gbank()
                      for i in range(GB):
                          S_.add('pe', lambda e, i=i, psX=psX, Qn=Qn: e.matmul(psX[:64, i * 64:(i + 1) * 64], Qn[:, i, :], X[:, i, :], start=True, stop=True),
                                 reads=[kQn, 'X'], writes=[kX])
                      S_.add('dve', lambda e, psX=psX: e.tensor_tensor(out=flat(X), in0=psX[:64, :], in1=flat(X), op=ALU.add), reads=[kX, 'X'], writes=['X'])
                      cur = nxt
                  chk(5)
                  bs = b64[:, ch0:ch0 + GB]
                  bes = bexp[:, ch0:ch0 + GB]
                  kds = kdf[:, ch0:ch0 + GB]
                  S_.add('pool', lambda e, bs=bs, v64_=v64_: e.tensor_tensor(out=vb[:], in0=v64_[:], in1=bc(bs, 128), op=ALU.mult), reads=[kk[3], 'b64'], writes=['vb'])
                  S_.add('dve', lambda e, bes=bes, k64_=k64_: e.tensor_tensor(out=kbg[:], in0=k64_[:], in1=bc(bes, 128), op=ALU.mult), reads=[kk[2], 'bexp'], writes=['kbg'])
                  S_.add('pool', lambda e, kds=kds, k64_=k64_: e.tensor_tensor(out=kd[:], in0=k64_[:], in1=bc(kds, 128), op=ALU.mult), reads=[kk[2], 'kdf'], writes=['kd'])
                  ubanks = []
                  for half in range(GB // 4):
                      bU, psU, kU = gbank()
                      ubanks.append((psU, kU))
                      for i4 in range(4):
                          i = half * 4 + i4
                          S_.add('pe', lambda e, i=i, i4=i4, psU=psU: e.matmul(psU[:64, i4 * 128:(i4 + 1) * 128], X[:, i, :], vb[:, i, :], start=True, stop=True),
                                 reads=['X', 'vb'], writes=[kU])
                  bW, psW, kW = gbank()
                  for i in range(GB):
                      S_.add('pe', lambda e, i=i, psW=psW: e.matmul(psW[:, i * 64:(i + 1) * 64], kbg[:, i, :], X[:, i, :], start=True, stop=True),
                             reads=['X', 'kbg'], writes=[kW])
                  for half, (psU, kU) in enumerate(ubanks):
                      S_.add('act', lambda e, half=half, psU=psU: e.activation(out=u_sb[:, half * 4:(half + 1) * 4, :].rearrange("p g c -> p (g c)"), in_=psU[:64, :], func=AF.Copy),
                             reads=[kU], writes=['u_sb'])
                  S_.add('act', lambda e, psW=psW: e.activation(out=wT_sb[:], in_=psW[:, :], func=AF.Copy), reads=[kW], writes=['wT_sb'])
                  chk(6)
                  oo = o_out[gi % 2]
                  kO = 'o_out%d' % (gi % 2)
                  for i in range(GB):
                      ch = ch0 + i
                      st = step[0]
                      Sc, Sn = Sst[st % 2], Sst[(st + 1) % 2]
                      kSc, kSn = 'Sst%d' % (st % 2), 'Sst%d' % ((st + 1) % 2)
                      vni, kvn = vn[st % 2], 'vn%d' % (st % 2)
                      oti, kot = o_tmp[st % 2], 'o_tmp%d' % (st % 2)
                      p1, k1 = qbank(0, st)
                      S_.add('pe', lambda e, i=i, p1=p1, Sc=Sc: e.matmul(p1[:64, :], wT_sb[:, i * 64:(i + 1) * 64], Sc[:], start=True, stop=True),
                             reads=['wT_sb', kSc], writes=[k1])
                      S_.add('dve', lambda e, i=i, p1=p1, vni=vni: e.tensor_tensor(out=vni[:], in0=u_sb[:, i, :], in1=p1[:64, :], op=ALU.subtract),
                             reads=['u_sb', k1], writes=[kvn])
                      p2, k2 = qbank(1, st)
                      S_.add('pe', lambda e, i=i, p2=p2, Sc=Sc, qT_=qT_: e.matmul(p2[:64, :], qT_[:, i * 64:(i + 1) * 64], Sc[:], start=True, stop=True),
                             reads=[kk[1], kSc], writes=[k2])
                      p4, k4 = qbank(3, st)
                      S_.add('pe', lambda e, i=i, p4=p4, vni=vni: e.matmul(p4[:, :], kd[:, i, :], vni[:], start=True, stop=True),
                             reads=['kd', kvn], writes=[k4])
                      p3, k3 = qbank(2, st)
                      S_.add('pe', lambda e, i=i, p3=p3, vni=vni: e.matmul(p3[:64, :], attnT[:, i, :], vni[:], start=True, stop=True),
                             reads=['attnT', kvn], writes=[k3])
                      S_.add('dve', lambda e, p4=p4, Sc=Sc, Sn=Sn, ch=ch: e.scalar_tensor_tensor(out=Sn[:], in0=Sc[:], scalar=egl[:, ch:ch + 1], in1=p4[:, :],
                                                                                                 op0=ALU.mult, op1=ALU.add),
                             reads=[kSc, 'egl', k4], writes=[kSn])
                      S_.add('act', lambda e, p2=p2, oti=oti, ch=ch: e.activation(out=oti[:], in_=p2[:64, :], func=AF.Copy, scale=eg[:, ch:ch + 1]),
                             reads=[k2, 'eg'], writes=[kot])
                      S_.add('dve', lambda e, i=i, p3=p3, oti=oti, oo=oo: e.tensor_tensor(out=oo[:, i, :], in0=oti[:], in1=p3[:64, :], op=ALU.add),
                             reads=[kot, k3], writes=[kO])
                      step[0] += 1
                  cx.dma_out(o64_d[hi, :, ch0:ch0 + GB, :], oo[:], [kO])
                  chk(7)
        except _Stop:
            pass
        cx.finish()
    return nc


def phase_b_inputs(qkvs, bgs, c):
    b = c // 4
    m = {k: [] for k in ('qT', 'kT', 'k64', 'v64', 'g64', 'b64')}
    for hi in range(NHB):
        h = (c % 4) * NHB + hi
        cat = lambda idx: np.concatenate([qkvs[b * 4 + qi][j][idx] for qi in range(4) for j in range(NJOB)], axis=-1)
        qT, kT, vT = cat(h), cat(8 + h), cat(16 + h)
        m['qT'].append(qT)
        m['kT'].append(kT)
        tm = lambda aT: np.ascontiguousarray(aT.T.reshape(NCH, 64, 128).transpose(1, 0, 2))
        m['k64'].append(tm(kT))
        m['v64'].append(tm(vT))
        gg = np.concatenate([bgs[b * 4 + qi][j][1, 8 + h] for qi in range(4) for j in range(NJOB)])
        bb = np.concatenate([bgs[b * 4 + qi][j][0, h] for qi in range(4) for j in range(NJOB)])
        m['g64'].append(np.ascontiguousarray(gg.reshape(NCH, 64).T))
        m['b64'].append(np.ascontiguousarray(bb.reshape(NCH, 64).T))
    return {k: np.ascontiguousarray(np.stack(v)).astype(np.float32) for k, v in m.items()}


TC = 512
NJC = 4
NCC = TC + HALO
NFP = 2
FPC = 44 // NFP
XH = 4


def build_phase_c(final):
    nc = bass.Bass("TRN2", target_bir_lowering=False)
    dt_in = lambda name, shape: nc.dram_tensor(name, shape, F32, kind="ExternalInput").ap()
    xT = dt_in("xT", [NJC, 128, KC, NCC])
    oT = dt_in("oT", [NJC, 8, 128, NCC])
    szT = dt_in("szT", [NJC, 8, 128, NCC])
    yscT = dt_in("yscT", [NJC, 8, 128, NCC])
    memT = dt_in("memT", [128, KC, NMEM])
    vecs_d = dt_in("vecs", [128, 1 + 4 * KC])
    fconv_d = dt_in("fconv", [128, 88 * 3])
    flag_d = dt_in("flag", [128, NJC])
    Wo_d = dt_in("Wo", [KC, 128, KC, 128])
    Wq_d = dt_in("Wq", [KC, 128, KC, 128])
    Wk_d = dt_in("Wk", [KC, 128, KC, 128])
    Wv_d = dt_in("Wv", [4, 128, KC, 512])
    Wxo_d = dt_in("Wxo", [KC, 128, KC, 128])
    Wup_d = dt_in("Wup", [88, 128, KC, 128])
    Wd_d = dt_in("Wd", [NFP, KC, 128, FPC, 128])
    xout = nc.dram_tensor("xout", [NJC, 128, KC, TC], F32, kind="ExternalOutput").ap()
    with ExitStack() as es:
        cx = Ctx(nc, es)
        S_ = cx.S
        sb = cx.sb
        ones = make_consts(cx)
        ones_b = sb('ones_b', [128, 128], BF16)
        S_.add('pool', lambda e: e.memset(ones_b[:], 1.0), writes=['ones_b'])
        xall = sb('xall', [128, KC, NCC], F32)
        A = sb('A', [128, KC, NCC], BF16)
        Bb = sb('Bb', [128, KC, NCC], BF16)
        act = sb('act', [128, FPC, NCC], BF16)
        vecs = sb('vecs_sb', [128, 1 + 4 * KC], F32)
        fconv = sb('fconv_sb', [128, 88 * 3], F32)
        KT = sb('KT', [128, KC, NMEM], BF16)
        V = sb('V', [128, 2, D], BF16)
        wbufs = [sb('wbC%d' % i, [128, KC, 128], BF16) for i in range(3)]
        wdbufs = [sb('wdC%d' % i, [128, FPC, 128], BF16) for i in range(2)]
        stgs = [sb('stg%d' % i, [128, NCC], F32) for i in range(4)]
        sqb = [sb('sqb%d' % i, [128, 512], F32) for i in range(2)]
        rnb = [sb('rnb%d' % i, [128, 512], F32) for i in range(2)]
        E = [sb('E%d' % i, [128, 2, NCC], BF16) for i in range(2)]
        rden = sb('rden', [128, NCC], F32)
        accs = [sb('acc%d' % i, [128, TC], F32) for i in range(4)]
        sgb = [sb('sg%d' % i, [128, TC], F32) for i in range(2)]
        tmpo = [sb('tmpo%d' % i, [128, NCC], F32) for i in range(2)]
        fin = [sb('fin%d' % i, [128, TC], F32) for i in range(2)]
        cx.dma_in('sp', vecs[:], vecs_d, ['vecs'])
        cx.dma_in('sp', fconv[:], fconv_d, ['fconv'])
        flag = sb('flag_sb', [128, NJC], F32)
        cx.dma_in('sp', flag[:], flag_d, ['flag'])
        gon = vecs[:, 0:1]
        wnx = vecs[:, 1:1 + KC]
        wnm = vecs[:, 1 + KC:1 + 2 * KC]
        wnf = vecs[:, 1 + 2 * KC:1 + 3 * KC]
        wnl = vecs[:, 1 + 3 * KC:1 + 4 * KC]
        Ak = ['A%d' % kc for kc in range(KC)]
        Bk = ['B%d' % kc for kc in range(KC)]
        wcnt = [0]
        wdcnt = [0]
        stg_rr = [0]

        def next_stg():
            i = stg_rr[0]
            stg_rr[0] = (i + 1) % len(stgs)
            return stgs[i], 'stg%d' % i

        for kc in range(KC):
            cx.dma_in('sp', xall[:, kc, 0:NMEM], memT[:, kc, :], ['xall'])
        emit_rmsnorm(cx, xall, 'xall', wnm, 'vecs', A, Ak, ones, NMEM, sqb, rnb)
        for oc in range(KC):
            def ev(ps, bkey, c0, n, oc=oc):
                S_.add('act', lambda e: e.activation(out=KT[:, oc, c0:c0 + n], in_=ps[:, :n], func=AF.Copy), reads=[bkey], writes=['KT'])
            emit_proj_chunk(cx, Wk_d[oc], wbufs, wcnt[0], A, Ak, KC, NMEM, None, None, wname='wC', evac=ev)
            wcnt[0] += 1
        for cb in range(4):
            wv = act[:, 0:KC, 0:512]
            cx.dma_in('pool', wv, Wv_d[cb], ['act'])
            for mb in range(2):
                bk = cx.next_bank()
                ps = cx.banks[bk]
                for kc in range(KC):
                    S_.add('pe', lambda e, kc=kc, mb=mb, ps=ps: e.matmul(ps[:, :], A[:, kc, mb * 128:(mb + 1) * 128], act[:, kc, 0:512],
                                                                       start=(kc == 0), stop=(kc == KC - 1)),
                           reads=[Ak[kc], 'act'], writes=['bank%d' % bk])
                S_.add('act', lambda e, mb=mb, cb=cb, ps=ps: e.activation(out=V[:, mb, cb * 512:(cb + 1) * 512], in_=ps[:, :], func=AF.Copy),
                       reads=['bank%d' % bk], writes=['V'])

        def resid_evac(col_off):
            def mk(oc):
                def ev(ps, bkey, c0, n):
                    S_.add('dve', lambda e: e.tensor_tensor(out=xall[:, oc, col_off + c0:col_off + c0 + n], in0=ps[:, :n],
                                                            in1=xall[:, oc, col_off + c0:col_off + c0 + n], op=ALU.add),
                           reads=[bkey, 'xall'], writes=['xall'])
                return ev
            return mk

        for job in range(NJC):
            for kc in range(KC):
                cx.dma_in('sp', xall[:, kc, :], xT[job, :, kc, :], ['xall'])
            for h in range(8):
                so, ko = next_stg()
                cx.dma_in('sp', so[:], oT[job, h], [ko])
                sz_, kz = next_stg()
                cx.dma_in('sp', sz_[:], szT[job, h], [kz])
                sq = tmpo[h % 2]
                ksq = 'tmpo%d' % (h % 2)
                S_.add('act', lambda e, sq=sq, so=so: e.activation(out=sq[:], in_=so[:], func=AF.Square), reads=[ko], writes=[ksq])
                for (c0, n) in col_tiles(NCC):
                    bk = cx.next_bank()
                    ps = cx.banks[bk]
                    S_.add('pe', lambda e, sq=sq, c0=c0, n=n, ps=ps: e.matmul(ps[:, :n], ones[:], sq[:, c0:c0 + n], start=True, stop=True),
                           reads=[ksq, 'ones'], writes=['bank%d' % bk])
                    S_.add('act', lambda e, c0=c0, n=n, ps=ps: e.activation(out=rden[:, c0:c0 + n], in_=ps[:, :n], func=AF.Ln, scale=1.0 / DH, bias=EPS),
                           reads=['bank%d' % bk], writes=['rden%d' % c0])
                    S_.add('act', lambda e, c0=c0, n=n: e.activation(out=rden[:, c0:c0 + n], in_=rden[:, c0:c0 + n], func=AF.Exp, scale=-0.5),
                           reads=['rden%d' % c0], writes=['rden%d' % c0])
                rk = ['rden%d' % c0 for (c0, n) in col_tiles(NCC)]
                S_.add('dve', lambda e, so=so: e.tensor_tensor(out=so[:], in0=so[:], in1=rden[:], op=ALU.mult), reads=[ko] + rk, writes=[ko])
                S_.add('dve', lambda e, so=so, sz_=sz_, h=h: e.scalar_tensor_tensor(out=A[:, h, :], in0=so[:], scalar=gon, in1=sz_[:], op0=ALU.mult, op1=ALU.mult),
                       reads=[ko, kz, 'vecs'], writes=[Ak[h]])
            for j in range(8):
                cx.dma_in('pool', A[:, 8 + j, :], yscT[job, j], [Ak[8 + j]])
            mk = resid_evac(0)
            for oc in range(KC):
                emit_proj_chunk(cx, Wo_d[oc], wbufs, wcnt[0], A, Ak, KC, NCC, None, None, wname='wC', evac=mk(oc))
                wcnt[0] += 1
            emit_rmsnorm(cx, xall, 'xall', wnx, 'vecs', Bb, Bk, ones, NCC, sqb, rnb)
            for oc in range(KC):
                def ev(ps, bkey, c0, n, oc=oc):
                    S_.add('act', lambda e: e.activation(out=A[:, oc, c0:c0 + n], in_=ps[:, :n], func=AF.Copy, scale=float((D // XH) ** -0.5)),
                           reads=[bkey], writes=[Ak[oc]])
                emit_proj_chunk(cx, Wq_d[oc], wbufs, wcnt[0], Bb, Bk, KC, NCC, None, None, wname='wC', evac=ev)
                wcnt[0] += 1
            for hh in range(XH):
                Eh = E[hh % 2]
                kE = 'E%d' % (hh % 2)
                for mb in range(2):
                    for (c0, n) in col_tiles(NCC):
                        bk = cx.next_bank()
                        ps = cx.banks[bk]
                        for dc in range(4):
                            ch = hh * 4 + dc
                            S_.add('pe', lambda e, ch=ch, mb=mb, c0=c0, n=n, ps=ps, dc=dc: e.matmul(
                                ps[:, :n], KT[:, ch, mb * 128:(mb + 1) * 128], A[:, ch, c0:c0 + n], start=(dc == 0), stop=(dc == 3)),
                                reads=['KT', Ak[ch]], writes=['bank%d' % bk])
                        S_.add('act', lambda e, mb=mb, c0=c0, n=n, ps=ps, Eh=Eh: e.activation(out=Eh[:, mb, c0:c0 + n], in_=ps[:, :n], func=AF.Exp),
                               reads=['bank%d' % bk], writes=[kE + '_%d' % mb])
                for (c0, n) in col_tiles(NCC):
                    bk = cx.next_bank()
                    ps = cx.banks[bk]
                    for mb in range(2):
                        S_.add('pe', lambda e, mb=mb, c0=c0, n=n, ps=ps, Eh=Eh: e.matmul(ps[:, :n], ones_b[:], Eh[:, mb, c0:c0 + n], start=(mb == 0), stop=(mb == 1)),
                               reads=['ones_b', kE + '_%d' % mb], writes=['bank%d' % bk])
                    S_.add('dve', lambda e, c0=c0, n=n, ps=ps: e.reciprocal(out=rden[:, c0:c0 + n], in_=ps[:, :n]), reads=['bank%d' % bk], writes=['rden%d' % c0])
                for dvc in range(4):
                    ch = hh * 4 + dvc
                    for (c0, n) in col_tiles(NCC):
                        bk = cx.next_bank()
                        ps = cx.banks[bk]
                        for mb in range(2):
                            S_.add('pe', lambda e, mb=mb, ch=ch, c0=c0, n=n, ps=ps, Eh=Eh: e.matmul(
                                ps[:, :n], V[:, mb, ch * 128:(ch + 1) * 128], Eh[:, mb, c0:c0 + n], start=(mb == 0), stop=(mb == 1)),
                                reads=['V', kE + '_%d' % mb], writes=['bank%d' % bk])
                        S_.add('dve', lambda e, ch=ch, c0=c0, n=n, ps=ps: e.tensor_tensor(out=Bb[:, ch, c0:c0 + n], in0=ps[:, :n], in1=rden[:, c0:c0 + n], op=ALU.mult),
                               reads=['bank%d' % bk, 'rden%d' % c0], writes=[Bk[ch]])
            for oc in range(KC):
                emit_proj_chunk(cx, Wxo_d[oc], wbufs, wcnt[0], Bb, Bk, KC, NCC, None, None, wname='wC', evac=mk(oc))
                wcnt[0] += 1
            S_.add('dve', lambda e, job=job: e.tensor_scalar(out=xall[:, :, 0:HALO], in0=xall[:, :, 0:HALO], scalar1=flag[:, job:job + 1], scalar2=None, op0=ALU.mult),
                   reads=['xall', 'flag'], writes=['xall'])
            emit_rmsnorm(cx, xall, 'xall', wnf, 'vecs', A, Ak, ones, NCC, sqb, rnb)
            mk3 = resid_evac(HALO)
            for fp in range(NFP):
                for jj in range(FPC):
                    j = fp * FPC + jj
                    cv = []
                    for which in range(2):
                        chn = 2 * j + which
                        stg, sk = next_stg()
                        emit_proj_chunk(cx, Wup_d[chn], wbufs, wcnt[0], A, Ak, KC, NCC, stg, sk, wname='wC')
                        wcnt[0] += 1
                        acc = accs[(2 * j + which) % 4]
                        ak = 'acc%d' % ((2 * j + which) % 4)
                        for t in range(3):
                            if t == 0:
                                S_.add('dve', lambda e, acc=acc, stg=stg, chn=chn, t=t: e.tensor_scalar(
                                    out=acc[:], in0=stg[:, 1 + t:1 + t + TC], scalar1=fconv[:, chn * 3 + t:chn * 3 + t + 1], scalar2=None, op0=ALU.mult),
                                    reads=[sk, 'fconv'], writes=[ak])
                            else:
                                S_.add('dve', lambda e, acc=acc, stg=stg, chn=chn, t=t: e.scalar_tensor_tensor(
                                    out=acc[:], in0=stg[:, 1 + t:1 + t + TC], scalar=fconv[:, chn * 3 + t:chn * 3 + t + 1], in1=acc[:],
                                    op0=ALU.mult, op1=ALU.add),
                                    reads=[sk, 'fconv', ak], writes=[ak])
                        cv.append((acc, ak))
                    sg = sgb[j % 2]
                    ksg = 'sg%d' % (j % 2)
                    S_.add('act', lambda e, sg=sg, a0=cv[0][0]: e.activation(out=sg[:], in_=a0[:], func=AF.Silu), reads=[cv[0][1]], writes=[ksg])
                    S_.add('pool', lambda e, sg=sg, a1=cv[1][0], jj=jj: e.tensor_tensor(out=act[:, jj, 0:TC], in0=sg[:], in1=a1[:], op=ALU.mult),
                           reads=[ksg, cv[1][1]], writes=['act'])
                for oc in range(KC):
                    slot = wdcnt[0] % 2
                    wdcnt[0] += 1
                    wb = wdbufs[slot]
                    wk = 'wdC%d' % slot
                    cx.dma_in('pool', wb[:], Wd_d[fp, oc], [wk])
                    bk = cx.next_bank()
                    ps = cx.banks[bk]
                    for jj in range(FPC):
                        S_.add('pe', lambda e, jj=jj, wb=wb, ps=ps: e.matmul(ps[:, :TC], wb[:, jj, :], act[:, jj, 0:TC], start=(jj == 0), stop=(jj == FPC - 1)),
                               reads=[wk, 'act'], writes=['bank%d' % bk])
                    mk3(oc)(ps, 'bank%d' % bk, 0, TC)
            if final:
                def outf(kc, c0, n, rn, rnk):
                    if c0 != 0:
                        return
                    fb = fin[kc % 2]
                    kf = 'fin%d' % (kc % 2)
                    pass
                xv = xall[:, :, HALO:]
                bk = cx.next_bank()
                ps = cx.banks[bk]
                for kc in range(KC):
                    sqx = sqb[kc % 2]
                    sqk = 'rms_sq%d' % (kc % 2)
                    S_.add('act', lambda e, sqx=sqx, kc=kc: e.activation(out=sqx[:, :TC], in_=xall[:, kc, HALO:], func=AF.Square), reads=['xall'], writes=[sqk])
                    S_.add('pe', lambda e, sqx=sqx, kc=kc, ps=ps: e.matmul(ps[:, :TC], ones[:], sqx[:, :TC], start=(kc == 0), stop=(kc == KC - 1)),
                           reads=[sqk, 'ones'], writes=['bank%d' % bk])
                rn = rnb[0]
                S_.add('act', lambda e, rn=rn, ps=ps: e.activation(out=rn[:, :TC], in_=ps[:, :TC], func=AF.Ln, scale=1.0 / D, bias=EPS), reads=['bank%d' % bk], writes=['rms_rn0'])
                S_.add('act', lambda e, rn=rn: e.activation(out=rn[:, :TC], in_=rn[:, :TC], func=AF.Exp, scale=-0.5), reads=['rms_rn0'], writes=['rms_rn0'])
                for kc in range(KC):
                    fb = fin[kc % 2]
                    kf = 'fin%d' % (kc % 2)
                    S_.add('dve', lambda e, fb=fb, kc=kc, rn=rn: e.scalar_tensor_tensor(out=fb[:], in0=xall[:, kc, HALO:], scalar=wnl[:, kc:kc + 1], in1=rn[:, :TC],
                                                                                      op0=ALU.mult, op1=ALU.mult),
                           reads=['xall', 'rms_rn0', 'vecs'], writes=[kf])
                    cx.dma_out(xout[job, :, kc, :], fb[:], [kf])
            else:
                for kc in range(KC):
                    cx.dma_out(xout[job, :, kc, :], xall[:, kc, HALO:], ['xall'])
        cx.finish()
    return nc


def phase_c_params(inp, l):
    cols16 = [(oc * 128, 128) for oc in range(KC)]
    p = {}
    p['Wo'] = wchunks(inp['w_mix_out'][l], cols16)
    p['Wq'] = wchunks(inp['w_xq'][l], cols16)
    p['Wk'] = wchunks(inp['w_xk'][l], cols16)
    p['Wxo'] = wchunks(inp['w_xo'][l], cols16)
    wv = inp['w_xv'][l]
    p['Wv'] = np.ascontiguousarray(wv.reshape(KC, 128, 4, 512).transpose(2, 1, 0, 3))
    upcols = []
    for j in range(44):
        upcols += [(j * 128, 128), (DFF + j * 128, 128)]
    p['Wup'] = wchunks(inp['w_ffn_up'][l], upcols)
    wd = inp['w_ffn_down'][l]
    p['Wd'] = np.ascontiguousarray(wd.reshape(NFP, FPC, 128, KC, 128).transpose(0, 3, 2, 1, 4))
    fc = inp['ffn_conv'][l]
    fconv = np.zeros((128, 88, 3), np.float32)
    for ci, (c0, n) in enumerate(upcols):
        fconv[:, ci, :] = fc[:, c0:c0 + 128].T
    p['fconv'] = fconv.reshape(128, 88 * 3)
    vecs = np.zeros((128, 1 + 4 * KC), np.float32)
    vecs[:, 0] = inp['gdn_out_norm'][l]
    fm = lambda v: v.reshape(KC, 128).T
    vecs[:, 1:1 + KC] = fm(inp['xattn_norm'][l])
    vecs[:, 1 + KC:1 + 2 * KC] = fm(inp['mem_norm'][l])
    vecs[:, 1 + 2 * KC:1 + 3 * KC] = fm(inp['ffn_norm'][l])
    vecs[:, 1 + 3 * KC:1 + 4 * KC] = fm(inp['final_norm'])
    p['vecs'] = vecs
    return p


_PROGS = {}


def _prog(name):
    if name not in _PROGS:
        if name == 'a':
            _PROGS[name] = build_phase_a()
        elif name == 'b':
            _PROGS[name] = build_phase_b()
        elif name == 'c':
            _PROGS[name] = build_phase_c(False)
        else:
            _PROGS[name] = build_phase_c(True)
    return _PROGS[name]


def _jobs_c(a_b, t0):
    return np.stack([to_fm(job_cols(a_b, t0 + j * TC, HALO, TC)) for j in range(NJC)])


def kernel(**inp):
    inp = {k: np.asarray(v, dtype=np.float32) for k, v in inp.items()}
    x = np.array(inp['x'], dtype=np.float32, copy=True)
    mem = inp['mem']
    cores = list(range(NCORE))
    memT = [to_fm(mem[b]) for b in range(B)]
    for l in range(DEPTH):
        pa = phase_a_params(inp, l)
        in_maps = []
        for c in cores:
            b, t0 = core_tokens(c)
            xT = np.stack([to_fm(job_cols(x[b], t0 + j * TJ)) for j in range(NJOB)])
            in_maps.append(dict(xT=xT, wn=pa['wn'], W=pa['W'], gconv=pa['gconv'], scconv=pa['scconv'], small=pa['small']))
        ra = run_bass_kernel_spmd(_prog('a'), in_maps, core_ids=cores).results
        del in_maps, pa
        qkvs = [r['qkv'] for r in ra]
        bgs = [r['bg'] for r in ra]
        in_maps = [phase_b_inputs(qkvs, bgs, c) for c in cores]
        rb = run_bass_kernel_spmd(_prog('b'), in_maps, core_ids=cores).results
        del in_maps
        o_tok = np.zeros((B, S, H * DH), np.float32)
        for c in cores:
            b = c // 4
            for hi in range(NHB):
                h = (c % 4) * NHB + hi
                o_tok[b, :, h * DH:(h + 1) * DH] = rb[c]['o64'][hi].transpose(1, 0, 2).reshape(S, DH)
        sz_tok = np.zeros((B, S, H * DH), np.float32)
        ysc_tok = np.zeros((B, S, H * DH), np.float32)
        for c in cores:
            b, t0 = core_tokens(c)
            for j in range(NJOB):
                sl = slice(t0 + j * TJ, t0 + (j + 1) * TJ)
                sz_tok[b, sl] = ra[c]['sz'][j].transpose(2, 0, 1).reshape(TJ, H * DH)
                ysc_tok[b, sl] = ra[c]['ysc'][j].transpose(2, 0, 1).reshape(TJ, H * DH)
        pc = phase_c_params(inp, l)
        in_maps = []
        for c in cores:
            b, t0 = core_tokens(c)
            m = dict(pc)
            m['xT'] = _jobs_c(x[b], t0)
            m['oT'] = np.ascontiguousarray(_jobs_c(o_tok[b], t0).transpose(0, 2, 1, 3))
            m['szT'] = np.ascontiguousarray(_jobs_c(sz_tok[b], t0).transpose(0, 2, 1, 3))
            m['yscT'] = np.ascontiguousarray(_jobs_c(ysc_tok[b], t0).transpose(0, 2, 1, 3))
            m['memT'] = memT[b]
            fl = np.ones((128, NJC), np.float32)
            if t0 == 0:
                fl[:, 0] = 0.0
            m['flag'] = fl
            in_maps.append(m)
        rc = run_bass_kernel_spmd(_prog('cf' if l == DEPTH - 1 else 'c'), in_maps, core_ids=cores).results
        del in_maps, pc
        for c in cores:
            b, t0 = core_tokens(c)
            x[b, t0:t0 + NJC * TC] = rc[c]['xout'].transpose(0, 3, 2, 1).reshape(NJC * TC, D)
    return x
```
